# Optimizing a Trainium2 kernel written in Bass

```python
import math
import jax, jax.numpy as jnp
from jax import lax
import numpy as np

D_MODEL = 1024
BATCH = 2
SEQ = 16384
DEPTH = 2

D_FF = 2816
A_CHUNK = 128
A_GROUPS = 4
A_GROUP_DIM = 128
A_WIDTH = A_GROUPS * A_GROUP_DIM
B_WINDOW = 128
B_Q_HEADS = 8
B_KV_HEADS = 2
B_HEAD_DIM = 64
B_Q_PER_KV = B_Q_HEADS // B_KV_HEADS
B_WIDTH = B_Q_HEADS * B_HEAD_DIM
B_KV_WIDTH = B_KV_HEADS * B_HEAD_DIM
C_CHUNK = 128
C_HEADS = 4
C_QK_DIM = 128
C_V_DIM = 256
C_QK_WIDTH = C_HEADS * C_QK_DIM
C_V_WIDTH = C_HEADS * C_V_DIM
SPLIT_SIZES = (A_WIDTH, A_WIDTH,
               B_WIDTH, B_KV_WIDTH, B_KV_WIDTH,
               C_QK_WIDTH, C_QK_WIDTH, C_V_WIDTH, C_V_WIDTH,
               3 * D_MODEL)
IN_WIDTH = sum(SPLIT_SIZES)
NORM_EPS = 1e-6
GN_EPS = 1e-5

kernel_name = "hybrid_gmlp_swa_retention_macaron"


def rmsnorm(x, g):
    xf = x.astype(jnp.float32)
    y = xf * lax.rsqrt(jnp.mean(xf * xf, axis=-1, keepdims=True) + NORM_EPS)
    return (y * g.astype(jnp.float32)).astype(x.dtype)


def swiglu(h, w_in, w_out):
    a, b = jnp.split(h @ w_in, 2, axis=-1)
    return (jax.nn.silu(a) * b) @ w_out


def gmlp_spatial(z, v_norm, w_s, b_s):
    bsz, t_len, _ = z.shape
    u, v = jnp.split(z, 2, axis=-1)
    v = rmsnorm(v, v_norm)
    vc = v.reshape(bsz, t_len // A_CHUNK, A_CHUNK, A_GROUPS, A_GROUP_DIM)
    causal = jnp.tril(jnp.ones((A_CHUNK, A_CHUNK), dtype=bool))
    w = jnp.where(causal[None], w_s, jnp.zeros_like(w_s))
    s = jnp.einsum('gts,bcsgd->bctgd', w, vc) + b_s.T[None, None, :, :, None]
    return u * s.reshape(bsz, t_len, A_WIDTH)


def swa_attention(q, k, v, sinks):
    bsz, t_len, _ = q.shape
    n = t_len // B_WINDOW
    W = B_WINDOW
    qb = q.reshape(bsz, n, W, B_KV_HEADS, B_Q_PER_KV, B_HEAD_DIM)
    kb = k.reshape(bsz, n, W, B_KV_HEADS, B_HEAD_DIM)
    vb = v.reshape(bsz, n, W, B_KV_HEADS, B_HEAD_DIM)
    pad = ((0, 0), (1, 0), (0, 0), (0, 0), (0, 0))
    kk = jnp.concatenate([jnp.pad(kb, pad)[:, :-1], kb], axis=2)
    vv = jnp.concatenate([jnp.pad(vb, pad)[:, :-1], vb], axis=2)
    scores = jnp.einsum('bnqhgd,bnkhd->bnhgqk', qb, kk).astype(jnp.float32)
    scores = scores * (1.0 / math.sqrt(B_HEAD_DIM))
    qpos = jnp.arange(W)[:, None] + W
    kpos = jnp.arange(2 * W)[None, :]
    dist = (qpos - kpos).astype(jnp.float32)
    valid = (dist >= 0) & (dist < W)
    block_valid = valid[None] & ((jnp.arange(n) > 0)[:, None, None] | (kpos >= W)[None])
    slopes = jnp.exp2(-8.0 / B_Q_HEADS * (jnp.arange(B_Q_HEADS, dtype=jnp.float32) + 1.0))
    alibi = -slopes.reshape(B_KV_HEADS, B_Q_PER_KV)[:, :, None, None] * dist
    scores = jnp.where(block_valid[None, :, None, None], scores + alibi[None, None], -jnp.inf)
    sink = sinks.astype(jnp.float32).reshape(B_KV_HEADS, B_Q_PER_KV)[None, None, :, :, None, None]
    m = jnp.maximum(jnp.max(scores, axis=-1, keepdims=True), sink)
    p = jnp.exp(scores - m)
    probs = p / (jnp.sum(p, axis=-1, keepdims=True) + jnp.exp(sink - m))
    out = jnp.einsum('bnhgqk,bnkhd->bnqhgd', probs.astype(v.dtype), vv)
    return out.reshape(bsz, t_len, B_WIDTH)


def retention(q, k, v, g, gn_gain):
    bsz, t_len, _ = q.shape
    n = t_len // C_CHUNK
    dt = q.dtype
    log_gamma = jnp.log(1.0 - jnp.exp2(-5.0 - jnp.arange(C_HEADS, dtype=jnp.float32)))
    qc = q.reshape(bsz, n, C_CHUNK, C_HEADS, C_QK_DIM)
    kc = k.reshape(bsz, n, C_CHUNK, C_HEADS, C_QK_DIM) * (C_QK_DIM ** -0.5)
    vc = v.reshape(bsz, n, C_CHUNK, C_HEADS, C_V_DIM)
    idx = jnp.arange(C_CHUNK, dtype=jnp.float32)
    dist = idx[:, None] - idx[None, :]
    decay = jnp.where(dist[None] >= 0, jnp.exp(log_gamma[:, None, None] * jnp.maximum(dist, 0.0)[None]), 0.0)
    scores = jnp.einsum('bnthd,bnshd->bnhts', qc, kc) * decay.astype(dt)[None, None]
    inner = jnp.einsum('bnhts,bnshe->bnthe', scores, vc)
    k_w = jnp.exp(log_gamma[None, :] * (C_CHUNK - 1.0 - idx)[:, None]).astype(dt)
    kv = jnp.einsum('bnshd,bnshe->nbhde', kc * k_w[None, None, :, :, None], vc)
    chunk_decay = jnp.exp(log_gamma * C_CHUNK).astype(dt)[None, :, None, None]

    def step(state, kv_i):
        return chunk_decay * state + kv_i, state

    init = jnp.zeros((bsz, C_HEADS, C_QK_DIM, C_V_DIM), dtype=kv.dtype)
    _, s_prev = lax.scan(step, init, kv)
    q_w = jnp.exp(log_gamma[None, :] * (idx + 1.0)[:, None]).astype(dt)
    cross = jnp.einsum('bnthd,nbhde->bnthe', qc * q_w[None, None, :, :, None], s_prev)
    o = (inner + cross).astype(jnp.float32)
    mu = jnp.mean(o, axis=-1, keepdims=True)
    var = jnp.mean(jnp.square(o - mu), axis=-1, keepdims=True)
    o = ((o - mu) * lax.rsqrt(var + GN_EPS)).reshape(bsz, t_len, C_V_WIDTH)
    o = (o * gn_gain.astype(jnp.float32)).astype(dt)
    return o * jax.nn.silu(g)


def hybrid_layer(x, ffn1_norm, ffn1_w_in, ffn1_w_out, mix_norm, w_in, b_gate,
                 gmlp_v_norm, gmlp_w_s, gmlp_b_s, attn_sinks, ret_gn,
                 w_branch_a, w_branch_b, w_branch_c, w_out,
                 ffn2_norm, ffn2_w_in, ffn2_w_out):
    x = x + 0.5 * swiglu(rmsnorm(x, ffn1_norm), ffn1_w_in, ffn1_w_out)
    h = rmsnorm(x, mix_norm)
    p = h @ w_in
    cuts = list(np.cumsum(SPLIT_SIZES)[:-1])
    a_u, a_v, b_q, b_k, b_v, c_q, c_k, c_v, c_g, gate_pre = jnp.split(p, cuts, axis=-1)
    y_a = gmlp_spatial(jax.nn.gelu(jnp.concatenate([a_u, a_v], axis=-1)),
                       gmlp_v_norm, gmlp_w_s, gmlp_b_s) @ w_branch_a
    y_b = swa_attention(b_q, b_k, b_v, attn_sinks) @ w_branch_b
    y_c = retention(c_q, c_k, c_v, c_g, ret_gn) @ w_branch_c
    g_a, g_b, g_c = jnp.split(jax.nn.sigmoid(gate_pre + b_gate), 3, axis=-1)
    merged = g_a * y_a + g_b * y_b + g_c * y_c
    x = x + merged @ w_out
    x = x + 0.5 * swiglu(rmsnorm(x, ffn2_norm), ffn2_w_in, ffn2_w_out)
    return x


def setup_inputs(seed: int = 0) -> dict:
    key = jax.random.key(seed)
    ks = jax.random.split(key, 24)
    L, D, F = DEPTH, D_MODEL, D_FF
    f32 = jnp.float32

    def nrm(k, shape, fan_in):
        return jax.random.normal(k, shape, f32) * (fan_in ** -0.5)

    def gain(k, shape):
        return 1.0 + 0.02 * jax.random.normal(k, shape, f32)

    return {
        "x": jax.random.normal(ks[0], (BATCH, SEQ, D), f32),
        "ffn1_norm": gain(ks[1], (L, D)),
        "ffn1_w_in": nrm(ks[2], (L, D, 2 * F), D),
        "ffn1_w_out": nrm(ks[3], (L, F, D), F),
        "mix_norm": gain(ks[4], (L, D)),
        "w_in": nrm(ks[5], (L, D, IN_WIDTH), D),
        "b_gate": 0.02 * jax.random.normal(ks[6], (L, 3 * D), f32),
        "gmlp_v_norm": gain(ks[7], (L, A_WIDTH)),
        "gmlp_w_s": nrm(ks[8], (L, A_GROUPS, A_CHUNK, A_CHUNK), A_CHUNK),
        "gmlp_b_s": 1.0 + 0.02 * jax.random.normal(ks[9], (L, A_GROUPS, A_CHUNK), f32),
        "attn_sinks": 0.5 * jax.random.normal(ks[10], (L, B_Q_HEADS), f32),
        "ret_gn": gain(ks[11], (L, C_V_WIDTH)),
        "w_branch_a": nrm(ks[12], (L, A_WIDTH, D), A_WIDTH),
        "w_branch_b": nrm(ks[13], (L, B_WIDTH, D), B_WIDTH),
        "w_branch_c": nrm(ks[14], (L, C_V_WIDTH, D), C_V_WIDTH),
        "w_out": nrm(ks[15], (L, D, D), D),
        "ffn2_norm": gain(ks[16], (L, D)),
        "ffn2_w_in": nrm(ks[17], (L, D, 2 * F), D),
        "ffn2_w_out": nrm(ks[18], (L, F, D), F),
        "final_norm": gain(ks[19], (D,)),
    }


def reference(x, ffn1_norm, ffn1_w_in, ffn1_w_out, mix_norm, w_in, b_gate,
              gmlp_v_norm, gmlp_w_s, gmlp_b_s, attn_sinks, ret_gn,
              w_branch_a, w_branch_b, w_branch_c, w_out,
              ffn2_norm, ffn2_w_in, ffn2_w_out, final_norm):
    for l in range(DEPTH):
        x = hybrid_layer(x, ffn1_norm[l], ffn1_w_in[l], ffn1_w_out[l], mix_norm[l],
                         w_in[l], b_gate[l], gmlp_v_norm[l], gmlp_w_s[l], gmlp_b_s[l],
                         attn_sinks[l], ret_gn[l], w_branch_a[l], w_branch_b[l],
                         w_branch_c[l], w_out[l], ffn2_norm[l], ffn2_w_in[l], ffn2_w_out[l])
    return rmsnorm(x, final_norm)
```

```python
import math
import numpy as np
import concourse.bass as bass
import concourse.mybir as mybir
from concourse.bass_utils import run_bass_kernel_spmd

F32 = mybir.dt.float32
BF16 = mybir.dt.bfloat16
ALU = mybir.AluOpType
AF = mybir.ActivationFunctionType
AX = mybir.AxisListType

PE, ACT, DVE, POOL, SP = "tensor", "scalar", "vector", "gpsimd", "sync"
ENGS = (PE, ACT, DVE, POOL, SP)

L = 2
D = 1024
FF = 2816
NJ = FF // 128
TOK = 4096
TS = 1024
NT = TS // 512
NCH = TS // 128
NST = TOK // TS
NCORE = 8
GW = 1408
NEG = -30000.0
NORM_EPS = 1e-6
GN_EPS = 1e-5
USE_CC = True
STAGES = ("pass1", "mixer", "ffn2")
MIX = "ABCM"
BLV = 9
BSUB = 9
NOALB = False
OBE = 4


class Inst:
    __slots__ = ("eng", "idx", "fn", "waits", "signal", "dma", "sem", "val")

    def __init__(self, eng, idx, fn, dma=False):
        self.eng, self.idx, self.fn, self.dma = eng, idx, fn, dma
        self.waits = []
        self.signal = False
        self.sem = None
        self.val = 0


class _Rec:
    def __getattr__(self, name):
        def f(*a, **k):
            self.call = (name, a, k)
            return self
        return f


def _bind(fn):
    r = _Rec()
    fn(r)
    name, a, k = r.call
    return lambda e: getattr(e, name)(*a, **k)


class Prog:
    NDMA = 24

    def __init__(self, nc):
        self.nc = nc
        self.streams = {e: [] for e in ENGS}
        self.lastw = {}
        self.readers = {}
        self.waited = {e: {} for e in ENGS}
        self.dma_rr = 0
        self.dma_rr_e = {}
        self.ncc = 0
        self.dma_last = [None] * self.NDMA
        self.dma_cnt = [0] * self.NDMA

    def _need(self, eng, inst, dep):
        if dep is None or dep is inst:
            return
        w = self.waited[eng]
        if dep.dma:
            k = ("dma", dep.sem)
            if w.get(k, 0) >= dep.val:
                return
            w[k] = dep.val
            inst.waits.append(dep)
        else:
            if dep.eng == eng and eng == PE:
                return
            if w.get(dep.eng, -1) >= dep.idx:
                return
            w[dep.eng] = dep.idx
            dep.signal = True
            inst.waits.append(dep)

    def _track(self, eng, inst, reads, writes):
        for k in reads:
            self._need(eng, inst, self.lastw.get(k))
        for k in writes:
            self._need(eng, inst, self.lastw.get(k))
            for r in self.readers.get(k, ()):
                self._need(eng, inst, r)
        for k in reads:
            rl = self.readers.setdefault(k, [])
            if not inst.dma:
                rl[:] = [r for r in rl if r.dma or r.eng != eng]
            rl.append(inst)
        for k in writes:
            self.lastw[k] = inst
            self.readers[k] = []

    def op(self, eng, fn, reads=(), writes=()):
        s = self.streams[eng]
        inst = Inst(eng, len(s), _bind(fn))
        self._track(eng, inst, reads, writes)
        s.append(inst)
        return inst

    def dma(self, eng, out, in_, reads=(), writes=(), fn=None):
        s = self.streams[eng]
        if fn is None:
            fn = lambda e: e.dma_start(out=out, in_=in_)
        inst = Inst(eng, len(s), _bind(fn), dma=True)
        half = self.NDMA // 2
        base = 0 if eng == POOL else half
        k = base + self.dma_rr_e.get(eng, 0)
        self.dma_rr_e[eng] = (self.dma_rr_e.get(eng, 0) + 1) % half
        inst.sem = k
        self.dma_cnt[k] += 1
        inst.val = 16 * self.dma_cnt[k]
        self._need(eng, inst, self.dma_last[k])
        self.dma_last[k] = inst
        self._track(eng, inst, reads, writes)
        s.append(inst)
        return inst

    def cc(self, eng, fn, reads=(), writes=()):
        s = self.streams[eng]
        inst = Inst(eng, len(s), _bind(fn), dma=True)
        inst.sem = self.NDMA + self.ncc
        self.ncc += 1
        inst.val = 1
        self._track(eng, inst, reads, writes)
        s.append(inst)
        return inst

    def wait_all(self, eng, insts):
        s = self.streams[eng]
        inst = Inst(eng, len(s), None)
        for d in insts:
            self._need(eng, inst, d)
        s.append(inst)
        return inst

    def emit(self):
        nc = self.nc
        sems = {e: nc.alloc_semaphore(name=f"c_{e}") for e in ENGS}
        dsems = [nc.alloc_semaphore(name=f"d_{i}") for i in range(self.NDMA)]
        dsems += [nc.alloc_semaphore(name=f"cc_{i}") for i in range(self.ncc)]
        for e in ENGS:
            c = 0
            for inst in self.streams[e]:
                if (not inst.dma) and inst.signal:
                    c += 1
                    inst.val = c
        streams = self.streams

        def run(engname, engobj):
            for inst in streams[engname]:
                for d in inst.waits:
                    if d.dma:
                        engobj.wait_ge(dsems[d.sem], d.val)
                    else:
                        engobj.wait_ge(sems[d.eng], d.val)
                if inst.fn is None:
                    continue
                r = inst.fn(engobj)
                if inst.dma and inst.sem >= self.NDMA:
                    r.then_inc(dsems[inst.sem])
                elif inst.dma:
                    r.then_inc(dsems[inst.sem], 16)
                elif inst.signal:
                    r.then_inc(sems[engname], 1)

        with nc.Block() as block:
            @block.sync
            def _(e):
                run(SP, e)

            @block.scalar
            def _(e):
                run(ACT, e)

            @block.vector
            def _(e):
                run(DVE, e)

            @block.gpsimd
            def _(e):
                run(POOL, e)

            @block.tensor
            def _(e):
                run(PE, e)


SPLIT = (512, 512, 512, 128, 128, 512, 512, 1024, 1024, 3072)
OFF = np.concatenate([[0], np.cumsum(SPLIT)])
O_AU, O_AV, O_BQ, O_BK, O_BV, O_CQ, O_CK, O_CV, O_CG, O_GT = [int(v) for v in OFF[:10]]
NFM = 42


def _fm_cols():
    cols = []
    for c in range(4):
        cols.append(np.arange(O_AU + 128 * c, O_AU + 128 * (c + 1)))
    for c in range(4):
        cols.append(np.arange(O_BQ + 128 * c, O_BQ + 128 * (c + 1)))
    for kvh in range(2):
        k = np.arange(O_BK + 64 * kvh, O_BK + 64 * (kvh + 1))
        cols.append(np.concatenate([k, k]))
    for c in range(4):
        cols.append(np.arange(O_CQ + 128 * c, O_CQ + 128 * (c + 1)))
    for c in range(4):
        cols.append(np.arange(O_CK + 128 * c, O_CK + 128 * (c + 1)))
    for c in range(24):
        cols.append(np.arange(O_GT + 128 * c, O_GT + 128 * (c + 1)))
    return cols


def _chunks_lhs(W, colsel):
    K = W.shape[0]
    Wr = W.reshape(K // 128, 128, -1)
    out = []
    for cols in colsel:
        out.append(np.ascontiguousarray(Wr[:, :, cols].transpose(1, 0, 2)).reshape(128, -1))
    return np.stack(out)


def _consts():
    c = {}
    s = np.arange(128)
    c["cmask"] = (s[None, :] >= s[:, None]).astype(np.float32)
    c["ident"] = np.eye(128, dtype=np.float32)
    q = np.arange(128)[:, None]
    k = np.arange(256)[None, :]
    dist = (q + 128 - k).astype(np.float64)
    valid = (dist >= 0) & (dist < 128)
    slopes = np.exp2(-(np.arange(8) + 1.0))
    al = np.where(valid[:, None, :], -slopes[None, :, None] * dist[:, None, :], NEG)
    c["alibi"] = al.reshape(128, 2048).astype(np.float32)
    al0 = np.where((valid & (k >= 128))[:, None, :], -slopes[None, :, None] * dist[:, None, :], NEG)
    c["alibi0"] = al0.reshape(128, 2048).astype(np.float32)
    lg = np.log(1.0 - np.exp2(-5.0 - np.arange(4, dtype=np.float64)))
    scale = 128.0 ** -0.5
    ss = np.arange(128, dtype=np.float64)
    dec = np.where(ss[None, None, :] >= ss[:, None, None],
                   np.exp(-lg[None, :, None] * (ss[:, None, None] + 1.0)) * scale, 0.0)
    c["decT"] = dec.reshape(128, 512).astype(np.float32)
    qw = np.exp(lg[None, :] * (ss[:, None] + 1.0))
    c["epsq"] = (GN_EPS / (qw * qw)).astype(np.float32)
    c["kw"] = (np.exp(lg[None, :] * (127.0 - ss[:, None])) * scale).astype(np.float32)
    c["cd"] = [float(np.exp(lg[h] * 128.0)) for h in range(4)]
    c["lg"] = lg
    return c


def _core_consts(core, cst):
    j = core % 4
    b = core // 4
    lg = cst["lg"]
    coef = np.zeros((128, 32), np.float32)
    for r in range(8):
        if r // 4 == b and r % 4 < j:
            for h in range(4):
                coef[:, r * 4 + h] = np.exp(lg[h] * 128.0 * 32.0 * (j - (r % 4) - 1))
    hsel = np.zeros((128, 8), np.float32)
    if j > 0:
        hsel[:, core - 1] = 1.0
    return {"coef": coef, "hsel": hsel, "alibi0": cst["alibi0"] if j == 0 else cst["alibi"]}


def build(depth=L, do_final=True, cst=None):
    cst = cst or _consts()
    nc = bass.Bass("TRN2", target_bir_lowering=False)

    def din(name, shape, dt=F32):
        return nc.dram_tensor(name, list(shape), dt, kind="ExternalInput").ap()

    xT = din("xT", [8, 128, TOK])
    gains = din("gains", [128, (3 * L + 1) * 8])
    ffn_win = din("ffn_win", [2 * L, NJ, 128, 2048])
    ffn_wout = din("ffn_wout", [2 * L, 8, 128, FF])
    mw_fm = din("mw_fm", [L, NFM, 128, 1024])
    mw_tm = din("mw_tm", [L, 6, 128, 4096])
    mw_attv = din("mw_attv", [L, 128, 1024])
    mw_br = din("mw_br", [L, 8, 128, 2048])
    mw_out = din("mw_out", [L, 8, 128, 1024])
    bgate = din("bgate", [128, L * 24])
    vng_d = din("vng", [L, 128, 512])
    bsb_d = din("bsb", [L, 128, 512])
    wsT_d = din("wsT", [L, 128, 512])
    sinkb_d = din("sinkb", [L, 128, 8])
    gnb_d = din("gnb", [L, 128, 1024])
    cmask_d = din("cmask", [128, 128])
    ident_d = din("ident", [128, 128])
    alibi_d = din("alibi", [128, 2048])
    alibi0_d = din("alibi0", [128, 2048])
    decT_d = din("decT", [128, 512])
    epsq_d = din("epsq", [128, 4])
    kw_d = din("kw", [128, 4])
    coef_d = din("coef", [128, 32])
    hsel_d = din("hsel", [128, 8])
    outT = nc.dram_tensor("outT", [8, 128, TOK], F32, kind="ExternalOutput").ap()
    xres = nc.dram_tensor("xres", [8, 128, TOK], F32, kind="Internal").ap()
    kvs_d = nc.dram_tensor("kvs", [TOK // 128, 128, 1536], BF16, kind="Internal").ap()
    gsrc = nc.dram_tensor("gsrc", [128, GW], F32, kind="Internal").ap()
    gdst = nc.dram_tensor("gdst", [NCORE * 128, GW], F32, kind="Internal").ap()

    def sb(name, cols, dt=F32, parts=128):
        return nc.alloc_sbuf_tensor("sb_" + name, [parts, cols], dt)

    hT = sb("hT", 8 * TS, BF16)
    xn = sb("xn", 8 * 512, F32)
    sq = sb("sq", 8 * 512, BF16)
    rstd = sb("rstd", 512, F32)
    NSLOT = 27
    AR = sb("arena", NSLOT * TS, BF16)
    ring = [sb(f"wr{i}", 4096, BF16) for i in range(4)]
    attv_w = sb("attv_w", 1024, BF16)
    xr = [sb(f"xr{i}", 512, F32) for i in range(2)]
    xo = [sb(f"xo{i}", 512, F32) for i in range(2)]
    sa = [sb(f"sa{i}", 512, BF16) for i in range(2)]
    gcol = sb("gcol", (3 * L + 1) * 8, F32)
    bgc = sb("bgc", L * 24, F32)
    ones_b = sb("ones_b", 128, BF16)
    ident_b = sb("ident_b", 128, BF16)
    cmask = sb("cmask", 128, F32)
    alb = sb("alb", 2048, BF16)
    decT = sb("decT", 512, F32)
    epsq = sb("epsq", 4, F32)
    kw = sb("kw", 4, F32)
    coef = sb("coef", 32, F32)
    hsel = sb("hsel", 8, F32)
    vng = sb("vng", 512, F32)
    bsb = sb("bsb", 512, F32)
    wsT_f = sb("wsT_f", 512, F32)
    wsT_b = sb("wsT_b", 512, BF16)
    sinkb = sb("sinkb", 8, F32)
    gnb = sb("gnb", 1024, F32)
    vn = sb("vn", 512, BF16)
    tmpA = sb("tmpA", 512, F32)
    st8 = sb("st8", 64, F32)
    vatt = [sb(f"vatt{i}", 128, BF16) for i in range(3)]
    spb = sb("spb", 1024, F32)
    pb = sb("pb", 1024, BF16)
    pT = sb("pT", 1024, BF16)
    obt = sb("obt", 512, BF16)
    kvt = [sb(f"kvt{i}", 1536, BF16) for i in range(2)]
    gg = sb("gg", 1024, BF16)
    onb = sb("onb", 1024, BF16)
    yct = sb("yct", 1024, BF16)
    sTb = sb("sTb", 512, BF16)
    S = sb("S", 1024, F32)
    Sb = sb("Sb", 1024, BF16)
    bst = sb("bst", 24, F32)
    mv = sb("mv", 8, F32)
    gs = [sb(f"gs{i}", TS, BF16) for i in range(2)]
    tmpm = [sb(f"tmpm{i}", 512, F32) for i in range(2)]
    acc = [sb(f"acc{i}", 512, F32) for i in range(NT)]
    avg, junk = tmpm[0], tmpm[1]
    halo = sb("halo", 384, F32)

    ps = [nc.alloc_psum_tensor(f"ps{i}", [128, 512], F32) for i in range(8)]

    P = Prog(nc)

    def MM(out, lhsT, rhs, start, stop):
        return lambda e: e.matmul(out, lhsT=lhsT, rhs=rhs, start=start, stop=stop)

    def TT(out, in0, in1, op):
        return lambda e: e.tensor_tensor(out=out, in0=in0, in1=in1, op=op)

    def TSC(out, in0, s1, s2, op0, op1=None):
        if op1 is None:
            return lambda e: e.tensor_scalar(out=out, in0=in0, scalar1=s1, scalar2=None, op0=op0)
        return lambda e: e.tensor_scalar(out=out, in0=in0, scalar1=s1, scalar2=s2, op0=op0, op1=op1)

    def STT(out, in0, scalar, in1, op0, op1):
        return lambda e: e.scalar_tensor_tensor(out=out, in0=in0, scalar=scalar, in1=in1, op0=op0, op1=op1)

    def CP(out, in_):
        return lambda e: e.tensor_copy(out=out, in_=in_)
    state = {"bank": 0, "ring": 0, "xr": 0, "sa": 0, "gs": 0, "tm": 0, "kvt": 0}

    pinned = set()

    def bank():
        while True:
            b = state["bank"]
            state["bank"] = (b + 1) % 8
            if b not in pinned:
                return b

    def PK(b):
        return ("ps", b)

    def AK(slot, t=None):
        if t is None:
            return [("A", slot, tt) for tt in range(NT)]
        return [("A", slot, t)]

    def A(slot, c0=0, c1=TS):
        return AR[:, slot * TS + c0: slot * TS + c1]

    def rr(name, n):
        v = state[name]
        state[name] = (v + 1) % n
        return v

    def wload(src_ap, ncols):
        i = rr("ring", 4)
        P.dma(POOL, ring[i][:, 0:ncols], src_ap, writes=[("w", i)])
        return i

    def act(eng_out, in_, func, reads, writes, eng=ACT, **kw_):
        P.op(eng, lambda e: e.activation(out=eng_out, in_=in_, func=func, **kw_), reads=reads, writes=writes)

    P.dma(SP, gcol[:], gains[:, :], writes=["gcol"])
    P.dma(SP, bgc[:], bgate[:, :], writes=["bgc"])
    P.dma(SP, cmask[:], cmask_d[:, :], writes=["cmask"])
    P.dma(POOL, ident_b[:], ident_d[:, :], writes=["ident_b"])
    P.dma(SP, decT[:], decT_d[:, :], writes=["decT"])
    P.dma(SP, epsq[:], epsq_d[:, :], writes=["epsq"])
    P.dma(SP, kw[:], kw_d[:, :], writes=["kw"])
    P.dma(SP, coef[:], coef_d[:, :], writes=["coef"])
    P.dma(SP, hsel[:], hsel_d[:, :], writes=["hsel"])
    P.op(DVE, lambda e: e.memset(ones_b[:], 1.0), writes=["ones_b"])

    def norm_sub(src, srckeys, st, t, gidx, final_dst=None):
        tok0 = st * TS + t * 512
        for k in range(8):
            P.dma(SP, xn[:, k * 512:(k + 1) * 512], src[k, :, tok0:tok0 + 512], reads=srckeys, writes=[("xn", k)])
        for k in range(8):
            act(sq[:, k * 512:(k + 1) * 512], xn[:, k * 512:(k + 1) * 512], AF.Square, [("xn", k)], [("sq", k)])
        b = bank()
        for k in range(8):
            P.op(PE, lambda e, k=k, b=b: e.matmul(ps[b][:], lhsT=ones_b[:], rhs=sq[:, k * 512:(k + 1) * 512],
                                                   start=(k == 0), stop=(k == 7)),
                 reads=[("sq", k), "ones_b"], writes=[PK(b)])
        P.op(DVE, lambda e, b=b: e.tensor_scalar(out=rstd[:], in0=ps[b][:], scalar1=1.0 / D, scalar2=NORM_EPS,
                                                 op0=ALU.mult, op1=ALU.add), reads=[PK(b)], writes=["rstd"])
        act(rstd[:], rstd[:], AF.Ln, ["rstd"], ["rstd"])
        act(rstd[:], rstd[:], AF.Exp, ["rstd"], ["rstd"], scale=-0.5)
        for k in range(8):
            g = gcol[:, gidx * 8 + k: gidx * 8 + k + 1]
            if final_dst is None:
                o = hT[:, k * TS + t * 512: k * TS + (t + 1) * 512]
                P.op(DVE, lambda e, k=k, g=g, o=o: e.scalar_tensor_tensor(out=o, in0=xn[:, k * 512:(k + 1) * 512], scalar=g,
                                                                         in1=rstd[:], op0=ALU.mult, op1=ALU.mult),
                     reads=[("xn", k), "rstd", "gcol"], writes=[("hT", t)])
            else:
                i = rr("xr", 2)
                P.op(DVE, lambda e, k=k, g=g, i=i: e.scalar_tensor_tensor(out=xo[i][:], in0=xn[:, k * 512:(k + 1) * 512], scalar=g,
                                                                         in1=rstd[:], op0=ALU.mult, op1=ALU.mult),
                     reads=[("xn", k), "rstd", "gcol"], writes=[("xo", i)])
                fin.append(P.dma(POOL, final_dst[k, :, tok0:tok0 + 512], xo[i][:], reads=[("xo", i)], writes=[("out", st, k, t)]))

    fin = []

    def XK(name, st):
        return [(name, st, m, t) for m in range(8) for t in range(NT)]

    def norm_st(src, srcname, st, gidx):
        for t in range(NT):
            norm_sub(src, XK(srcname, st), st, t, gidx)

    def resid(b, src, srcname, dst, dstname, st, m, t, scale):
        tok0 = st * TS + t * 512
        i = rr("xr", 2)
        P.dma(SP, xr[i][:], src[m, :, tok0:tok0 + 512], reads=[(srcname, st, m, t)], writes=[("xr", i)])
        P.op(DVE, lambda e: e.scalar_tensor_tensor(out=xo[i][:], in0=ps[b][:], scalar=scale, in1=xr[i][:],
                                                   op0=ALU.mult, op1=ALU.add),
             reads=[PK(b), ("xr", i)], writes=[("xo", i)])
        P.dma(POOL, dst[m, :, tok0:tok0 + 512], xo[i][:], reads=[("xo", i)], writes=[(dstname, st, m, t)])

    def fm_group(lhs_of, rhs_of, nk, reads):
        bs = [bank() for _ in range(NT)]
        for k in range(nk):
            for t in range(NT):
                P.op(PE, lambda e, k=k, t=t: e.matmul(ps[bs[t]][:], lhsT=lhs_of(k), rhs=rhs_of(k, t),
                                                       start=(k == 0), stop=(k == nk - 1)),
                     reads=reads(k, t), writes=[PK(bs[t])])
        return bs

    def hT_rhs(k, t):
        return hT[:, k * TS + t * 512: k * TS + (t + 1) * 512]

    def ffn_phase(fi, src, srcname, gidx):
        norm_st(src, srcname, 0, gidx)
        for st in range(NST):
            for j in range(NJ):
                wi = wload(ffn_win[fi, j, :, :], 2048)
                W = ring[wi]
                ba = fm_group(lambda k: W[:, k * 256: k * 256 + 128], hT_rhs, 8,
                              lambda k, t: [("w", wi), ("hT", t)])
                bb = fm_group(lambda k: W[:, k * 256 + 128: k * 256 + 256], hT_rhs, 8,
                              lambda k, t: [("w", wi), ("hT", t)])
                for t in range(NT):
                    si = rr("sa", 2)
                    act(sa[si][:], ps[ba[t]][:], AF.Silu, [PK(ba[t])], [("sa", si)])
                    P.op(DVE, lambda e, t=t, si=si, j=j, b=bb[t]: e.tensor_tensor(
                        out=A(j, t * 512, (t + 1) * 512), in0=ps[b][:], in1=sa[si][:], op=ALU.mult),
                        reads=[PK(bb[t]), ("sa", si)], writes=AK(j, t))
            if st + 1 < NST:
                norm_st(src, srcname, st + 1, gidx)
            for m in range(8):
                wi = wload(ffn_wout[fi, m, :, :], FF)
                W = ring[wi]
                bs = fm_group(lambda k: W[:, k * 128:(k + 1) * 128],
                              lambda k, t: A(k, t * 512, (t + 1) * 512), NJ,
                              lambda k, t: [("w", wi)] + AK(k, t))
                for t in range(NT):
                    resid(bs[t], src, srcname, xres, "xres", st, m, t, 0.5)

    def load_layer_consts(l):
        P.dma(SP, vng[:], vng_d[l, :, :], writes=["vng"])
        P.dma(SP, bsb[:], bsb_d[l, :, :], writes=["bsb"])
        P.dma(SP, wsT_f[:], wsT_d[l, :, :], writes=["wsT_f"])
        P.dma(SP, sinkb[:], sinkb_d[l, :, :], writes=["sinkb"])
        P.dma(SP, gnb[:], gnb_d[l, :, :], writes=["gnb"])
        P.dma(POOL, attv_w[:], mw_attv[l, :, :], writes=["attv_w"])
        for g in range(4):
            P.op(DVE, lambda e, g=g: e.tensor_tensor(out=wsT_b[:, g * 128:(g + 1) * 128], in0=wsT_f[:, g * 128:(g + 1) * 128],
                                                      in1=cmask[:], op=ALU.mult),
                 reads=["wsT_f", "cmask"], writes=["wsT_b"])

    def kv_update(kt, ktkey):
        for hp in range(2):
            b = bank()
            for hh in range(2):
                h = hp * 2 + hh
                P.op(PE, lambda e, h=h, hh=hh, b=b: e.matmul(ps[b][:, hh * 256:(hh + 1) * 256], lhsT=kt[:, h * 128:(h + 1) * 128],
                                                             rhs=kt[:, 512 + h * 256: 512 + (h + 1) * 256], start=True, stop=True),
                     reads=[ktkey], writes=[PK(b)])
            for hh in range(2):
                h = hp * 2 + hh
                P.op(DVE, lambda e, h=h, hh=hh, b=b: e.scalar_tensor_tensor(
                    out=S[:, h * 256:(h + 1) * 256], in0=S[:, h * 256:(h + 1) * 256], scalar=cst["cd"][h],
                    in1=ps[b][:, hh * 256:(hh + 1) * 256], op0=ALU.mult, op1=ALU.add),
                    reads=[PK(b), "S"], writes=["S"])

    def tm_group(n, W, wkey, ncols, c0=0, stride=512):
        b = bank()
        t = n // 4
        for k in range(8):
            P.op(PE, lambda e, k=k: e.matmul(ps[b][:, 0:ncols], lhsT=hT[:, k * TS + n * 128: k * TS + (n + 1) * 128],
                                             rhs=W[:, k * stride + c0: k * stride + c0 + ncols], start=(k == 0), stop=(k == 7)),
                 reads=[("hT", t), wkey], writes=[PK(b)])
        return b

    def pass1(l):
        gidx = l * 3 + 1
        P.op(DVE, lambda e: e.memset(S[:], 0.0), reads=["S"], writes=["S"])
        wk = wload(mw_tm[l, 3, :, :], 4096)
        wv0 = wload(mw_tm[l, 4, :, :], 4096)
        wv1 = wload(mw_tm[l, 5, :, :], 4096)
        for st in range(NST):
            norm_st(xres, "xres", st, gidx)
            for n in range(NCH):
                cg = st * NCH + n
                ki = rr("kvt", 2)
                kt = kvt[ki]
                for gi, (wi, c0) in enumerate(((wk, 0), (wv0, 512), (wv1, 1024))):
                    b = tm_group(n, ring[wi], ("w", wi), 512)
                    eng = ACT if gi != 1 else DVE
                    if gi == 0:
                        for h in range(4):
                            P.op(DVE, lambda e, h=h, b=b: e.tensor_scalar(out=kt[:, h * 128:(h + 1) * 128], in0=ps[b][:, h * 128:(h + 1) * 128],
                                                                         scalar1=kw[:, h:h + 1], scalar2=None, op0=ALU.mult),
                                 reads=[PK(b), "kw"], writes=[("kvt", ki)])
                    elif eng is ACT:
                        act(kt[:, c0:c0 + 512], ps[b][:], AF.Copy, [PK(b)], [("kvt", ki)])
                    else:
                        P.op(DVE, lambda e, b=b, c0=c0: e.tensor_copy(out=kt[:, c0:c0 + 512], in_=ps[b][:]),
                             reads=[PK(b)], writes=[("kvt", ki)])
                P.dma(SP, kvs_d[cg, :, :], kt[:], reads=[("kvt", ki)], writes=[("kvs", cg)])
                kv_update(kt, ("kvt", ki))
                if st == NST - 1 and n == NCH - 1:
                    for kvh in range(2):
                        wi = wload(mw_fm[l, 8 + kvh, :, :], 1024)
                        b = bank()
                        for k in range(8):
                            P.op(PE, lambda e, k=k, wi=wi, b=b: e.matmul(
                                ps[b][:, 0:128], lhsT=ring[wi][:, k * 128:(k + 1) * 128],
                                rhs=hT[:, k * TS + n * 128: k * TS + (n + 1) * 128], start=(k == 0), stop=(k == 7)),
                                reads=[("w", wi), ("hT", NT - 1)], writes=[PK(b)])
                        P.op(DVE, lambda e, b=b, kvh=kvh: e.tensor_copy(out=halo[:, kvh * 128:(kvh + 1) * 128], in_=ps[b][:, 0:128]),
                             reads=[PK(b)], writes=["halo"])
                    b = tm_group(n, attv_w, "attv_w", 128, stride=128)
                    P.op(DVE, lambda e, b=b: e.tensor_copy(out=halo[:, 256:384], in_=ps[b][:, 0:128]),
                         reads=[PK(b)], writes=["halo"])
        P.dma(SP, gsrc[:, 0:1024], S[:], reads=["S"], writes=["gsrc"])
        P.dma(SP, gsrc[:, 1024:GW], halo[:], reads=["halo"], writes=["gsrc"])
        if USE_CC:
            P.cc(POOL, lambda e: e.collective_compute("AllGather", ALU.bypass, replica_groups=[list(range(NCORE))],
                                                      ins=[gsrc[:, :]], outs=[gdst[:, :]]),
                 reads=["gsrc"], writes=["gdst"])
        P.op(DVE, lambda e: e.memset(S[:], 0.0), reads=["S"], writes=["S"])
        P.op(DVE, lambda e: e.memset(halo[:], 0.0), reads=["halo"], writes=["halo"])
        for r in range(NCORE if USE_CC else 0):
            for hf in range(2):
                P.dma(SP, xn[:, hf * 512:(hf + 1) * 512], gdst[r * 128:(r + 1) * 128, hf * 512:(hf + 1) * 512],
                      reads=["gdst"], writes=[("xn", hf)])
            P.dma(SP, xn[:, 1024:1408], gdst[r * 128:(r + 1) * 128, 1024:GW], reads=["gdst"], writes=[("xn", 2)])
            for h in range(4):
                P.op(DVE, lambda e, r=r, h=h: e.scalar_tensor_tensor(
                    out=S[:, h * 256:(h + 1) * 256], in0=xn[:, h * 256:(h + 1) * 256], scalar=coef[:, r * 4 + h: r * 4 + h + 1],
                    in1=S[:, h * 256:(h + 1) * 256], op0=ALU.mult, op1=ALU.add),
                    reads=[("xn", h // 2), "coef", "S"], writes=["S"])
            P.op(DVE, lambda e, r=r: e.scalar_tensor_tensor(out=halo[:], in0=xn[:, 1024:1408], scalar=hsel[:, r:r + 1],
                                                            in1=halo[:], op0=ALU.mult, op1=ALU.add),
                 reads=[("xn", 2), "hsel", "halo"], writes=["halo"])

    SL_A, SL_B, SL_C, SL_M, SL_K = 0, 4, 8, 16, 24
    KD = SL_K * TS

    def kd(kvh, c0, c1):
        base = KD + kvh * (128 + TS)
        return AR[:, base + c0: base + c1]

    def mixer_phase(l):
        gidx = l * 3 + 1
        P.op(ACT, lambda e: e.activation(out=Sb[:], in_=S[:], func=AF.Copy), reads=["S"], writes=["Sb"])
        for kvh in range(2):
            P.op(DVE, lambda e, kvh=kvh: e.tensor_copy(out=kd(kvh, 0, 128), in_=halo[:, kvh * 128:(kvh + 1) * 128]),
                 reads=["halo"], writes=[("A", SL_K + 0, 0)] if False else AK(SL_K) + AK(SL_K + 1) + AK(SL_K + 2))
        P.op(DVE, lambda e: e.tensor_copy(out=vatt[2][:], in_=halo[:, 256:384]), reads=["halo"], writes=[("vatt", 2)])
        P.dma(POOL, alb[:], alibi0_d[:, :], writes=["alb"])
        KDK = AK(SL_K) + AK(SL_K + 1) + AK(SL_K + 2)
        norm_st(xres, "xres", 0, gidx)
        for st in range(NST):
            for c in range(4 if "A" in MIX else 0):
                wi = wload(mw_fm[l, c, :, :], 1024)
                bs = fm_group(lambda k: ring[wi][:, k * 128:(k + 1) * 128], hT_rhs, 8, lambda k, t: [("w", wi), ("hT", t)])
                for t in range(NT):
                    act(A(SL_A + c, t * 512, (t + 1) * 512), ps[bs[t]][:], AF.Gelu_apprx_tanh, [PK(bs[t])], AK(SL_A + c, t))
            for c in range(4 if "B" in MIX else 0):
                wi = wload(mw_fm[l, 4 + c, :, :], 1024)
                bs = fm_group(lambda k: ring[wi][:, k * 128:(k + 1) * 128], hT_rhs, 8, lambda k, t: [("w", wi), ("hT", t)])
                for t in range(NT):
                    if t == 0:
                        act(A(SL_B + c, t * 512, (t + 1) * 512), ps[bs[t]][:], AF.Copy, [PK(bs[t])], AK(SL_B + c, t))
                    else:
                        P.op(DVE, lambda e, c=c, t=t, b=bs[t]: e.tensor_copy(out=A(SL_B + c, t * 512, (t + 1) * 512), in_=ps[b][:]),
                             reads=[PK(bs[t])], writes=AK(SL_B + c, t))
            for kvh in range(2 if "B" in MIX else 0):
                wi = wload(mw_fm[l, 8 + kvh, :, :], 1024)
                bs = fm_group(lambda k: ring[wi][:, k * 128:(k + 1) * 128], hT_rhs, 8, lambda k, t: [("w", wi), ("hT", t)])
                for t in range(NT):
                    P.op(DVE, lambda e, kvh=kvh, t=t, b=bs[t]: e.tensor_copy(out=kd(kvh, 128 + t * 512, 128 + (t + 1) * 512), in_=ps[b][:]),
                         reads=[PK(bs[t])], writes=KDK)
            for c in range(8 if "C" in MIX else 0):
                wi = wload(mw_fm[l, 10 + c, :, :], 1024)
                bs = fm_group(lambda k: ring[wi][:, k * 128:(k + 1) * 128], hT_rhs, 8, lambda k, t: [("w", wi), ("hT", t)])
                for t in range(NT):
                    if (c + t) % 2 == 0:
                        act(A(SL_C + c, t * 512, (t + 1) * 512), ps[bs[t]][:], AF.Copy, [PK(bs[t])], AK(SL_C + c, t))
                    else:
                        P.op(DVE, lambda e, c=c, t=t, b=bs[t]: e.tensor_copy(out=A(SL_C + c, t * 512, (t + 1) * 512), in_=ps[b][:]),
                             reads=[PK(bs[t])], writes=AK(SL_C + c, t))
            wav = wload(mw_tm[l, 0, :, :], 4096)
            wg0 = wload(mw_tm[l, 1, :, :], 4096)
            wg1 = wload(mw_tm[l, 2, :, :], 4096)
            def A2(n):
                    t = n // 4
                    b = tm_group(n, ring[wav], ("w", wav), 512)
                    act(avg[:], ps[b][:], AF.Gelu_apprx_tanh, [PK(b)], [("tmpm", 0)])
                    act(junk[:], avg[:], AF.Square, [("tmpm", 0)], [("tmpm", 1), "ssA"], accum_out=st8[:, 0:1])
                    P.op(DVE, lambda e: e.tensor_scalar(out=st8[:, 1:2], in0=st8[:, 0:1], scalar1=1.0 / 512, scalar2=NORM_EPS,
                                                        op0=ALU.mult, op1=ALU.add), reads=["ssA"], writes=["rsA"])
                    act(st8[:, 1:2], st8[:, 1:2], AF.Ln, ["rsA"], ["rsA"])
                    act(st8[:, 1:2], st8[:, 1:2], AF.Exp, ["rsA"], ["rsA"], scale=-0.5)
                    P.op(DVE, lambda e: e.scalar_tensor_tensor(out=vn[:], in0=avg[:], scalar=st8[:, 1:2], in1=vng[:],
                                                               op0=ALU.mult, op1=ALU.mult),
                         reads=[("tmpm", 0), "rsA", "vng"], writes=["vn"])
                    b = bank()
                    for g in range(4):
                        P.op(PE, lambda e, g=g, b=b: e.matmul(ps[b][:, g * 128:(g + 1) * 128], lhsT=vn[:, g * 128:(g + 1) * 128],
                                                             rhs=wsT_b[:, g * 128:(g + 1) * 128], start=True, stop=True),
                             reads=["vn", "wsT_b"], writes=[PK(b)])
                    P.op(DVE, lambda e, b=b: e.tensor_tensor(out=tmpA[:], in0=ps[b][:], in1=bsb[:], op=ALU.add),
                         reads=[PK(b), "bsb"], writes=["tmpA"])
                    for g in range(4):
                        P.op(POOL, lambda e, g=g, n=n: e.tensor_tensor(out=A(SL_A + g, n * 128, (n + 1) * 128),
                                                                       in0=tmpA[:, g * 128:(g + 1) * 128],
                                                                       in1=A(SL_A + g, n * 128, (n + 1) * 128), op=ALU.mult),
                             reads=["tmpA"] + AK(SL_A + g, t), writes=AK(SL_A + g, t))
            def B2(n):
                    t = n // 4
                    cg = st * NCH + n
                    vcur = cg % 3
                    vprev = (cg + 2) % 3
                    b = tm_group(n, attv_w, "attv_w", 128, stride=128)
                    act(vatt[vcur][:], ps[b][:, 0:128], AF.Copy, [PK(b)], [("vatt", vcur)])
                    bo = bank()
                    pinned.add(bo)
                    for hf in range(2):
                        bsc = [bank(), bank()]
                        for hh in range(4):
                            h = hf * 4 + hh
                            c, pb0 = h // 2, (h % 2) * 64
                            kvh = h // 4
                            P.op(PE, lambda e, c=c, pb0=pb0, kvh=kvh, hh=hh, n=n: e.matmul(
                                ps[bsc[hh % 2]][:, (hh // 2) * 256:(hh // 2 + 1) * 256],
                                lhsT=AR[pb0:pb0 + 64, (SL_B + c) * TS + n * 128:(SL_B + c) * TS + (n + 1) * 128],
                                rhs=AR[pb0:pb0 + 64, KD + kvh * (128 + TS) + n * 128: KD + kvh * (128 + TS) + n * 128 + 256],
                                start=True, stop=True),
                                reads=AK(SL_B + c, t) + KDK, writes=[PK(bsc[hh % 2])])
                        for hh in range(4):
                            P.op(DVE, lambda e, hh=hh, hf=hf, b=bsc[hh % 2]: e.scalar_tensor_tensor(
                                out=spb[:, hh * 256:(hh + 1) * 256], in0=ps[b][:, (hh // 2) * 256:(hh // 2 + 1) * 256], scalar=0.125,
                                in1=alb[:, (hf * 4 + hh) * 256:(hf * 4 + hh + 1) * 256], op0=ALU.mult, op1=ALU.add),
                                reads=[PK(bsc[hh % 2]), "alb"], writes=["spb"])
                        o8 = hf * 4
                        if BSUB < 2:
                            continue
                        for hh in range(4):
                            P.op(DVE, lambda e, o8=o8, hh=hh: e.tensor_reduce(out=st8[:, 8 + o8 + hh: 9 + o8 + hh], in_=spb[:, hh * 256:(hh + 1) * 256],
                                                                             axis=AX.X, op=ALU.max), reads=["spb"], writes=["mx"])
                        P.op(DVE, lambda e, o8=o8: e.tensor_tensor(out=st8[:, 8 + o8:12 + o8], in0=st8[:, 8 + o8:12 + o8],
                                                                  in1=sinkb[:, o8:o8 + 4], op=ALU.max), reads=["mx", "sinkb"], writes=["mx"])
                        P.op(DVE, lambda e, o8=o8: e.tensor_scalar(out=st8[:, 16 + o8:20 + o8], in0=st8[:, 8 + o8:12 + o8], scalar1=-1.0,
                                                                  scalar2=None, op0=ALU.mult), reads=["mx"], writes=["nmx"])
                        for hh in range(4):
                            act(pb[:, hh * 256:(hh + 1) * 256], spb[:, hh * 256:(hh + 1) * 256], AF.Exp, ["spb", "nmx"], ["pb", "rs"],
                                bias=st8[:, 16 + o8 + hh:17 + o8 + hh], scale=1.0, accum_out=st8[:, 24 + o8 + hh:25 + o8 + hh])
                        if BSUB < 3:
                            continue
                        P.op(DVE, lambda e, o8=o8: e.tensor_tensor(out=st8[:, 32 + o8:36 + o8], in0=sinkb[:, o8:o8 + 4],
                                                                  in1=st8[:, 8 + o8:12 + o8], op=ALU.subtract), reads=["mx", "sinkb"], writes=["es"])
                        act(st8[:, 32 + o8:36 + o8], st8[:, 32 + o8:36 + o8], AF.Exp, ["es"], ["es"])
                        P.op(DVE, lambda e, o8=o8: e.tensor_tensor(out=st8[:, 40 + o8:44 + o8], in0=st8[:, 32 + o8:36 + o8],
                                                                  in1=st8[:, 24 + o8:28 + o8], op=ALU.add), reads=["es", "rs"], writes=["den"])
                        P.op(DVE, lambda e, o8=o8: e.reciprocal(out=st8[:, 48 + o8:52 + o8], in_=st8[:, 40 + o8:44 + o8]),
                             reads=["den"], writes=[("rec", hf)])
                        if BLV < 3:
                            continue
                        btr = [bank(), bank()]
                        for hh in range(4):
                            for kb in range(2):
                                j = hh * 2 + kb
                                P.op(PE, lambda e, hh=hh, kb=kb, j=j: e.matmul(
                                    ps[btr[j // 4]][:, (j % 4) * 128:(j % 4 + 1) * 128],
                                    lhsT=pb[:, hh * 256 + kb * 128: hh * 256 + (kb + 1) * 128], rhs=ident_b[:], start=True, stop=True),
                                    reads=["pb", "ident_b"], writes=[PK(btr[j // 4])])
                        act(pT[:, 0:512], ps[btr[0]][:], AF.Copy, [PK(btr[0])], ["pT"])
                        P.op(DVE, lambda e, b=btr[1]: e.tensor_copy(out=pT[:, 512:1024], in_=ps[b][:]), reads=[PK(btr[1])], writes=["pT"])
                        for hh in range(4):
                            h = hf * 4 + hh
                            kvh = h // 4
                            for kb in range(2):
                                vt = vatt[vprev] if kb == 0 else vatt[vcur]
                                vkey = ("vatt", vprev if kb == 0 else vcur)
                                P.op(PE, lambda e, h=h, hh=hh, kb=kb, vt=vt, kvh=kvh: e.matmul(
                                    ps[bo][:, h * 64:(h + 1) * 64], lhsT=pT[:, (hh * 2 + kb) * 128:(hh * 2 + kb + 1) * 128],
                                    rhs=vt[:, kvh * 64:(kvh + 1) * 64], start=(kb == 0), stop=(kb == 1)),
                                    reads=["pT", vkey], writes=[PK(bo)])
                        for hh in range(4):
                            h = hf * 4 + hh
                            P.op(DVE, lambda e, h=h: e.tensor_scalar(out=obt[:, h * 64:(h + 1) * 64], in0=ps[bo][:, h * 64:(h + 1) * 64],
                                                                     scalar1=st8[:, 48 + h:49 + h], scalar2=None, op0=ALU.mult),
                                 reads=[PK(bo), ("rec", hf)], writes=["obt"])
                    pinned.discard(bo)
                    if BLV < 4:
                        return
                    b = bank()
                    for c in range(4):
                        P.op(PE, lambda e, c=c, b=b: e.matmul(ps[b][:, c * 128:(c + 1) * 128], lhsT=obt[:, c * 128:(c + 1) * 128],
                                                             rhs=ident_b[:], start=True, stop=True),
                             reads=["obt", "ident_b"], writes=[PK(b)])
                    for c in range(4 if OBE >= 1 else 0):
                        eng = ACT if c % 2 == 0 else DVE
                        if OBE == 1:
                            dst, dkey = pb[:, c * 128:(c + 1) * 128], ["pb"]
                        else:
                            dst, dkey = A(SL_B + c, n * 128, (n + 1) * 128), AK(SL_B + c, t)
                        if OBE == 3:
                            eng = DVE
                        if OBE == 4:
                            eng = ACT
                        if eng is ACT:
                            act(dst, ps[b][:, c * 128:(c + 1) * 128], AF.Copy, [PK(b)], dkey)
                        else:
                            P.op(DVE, lambda e, c=c, n=n, b=b, dst=dst: e.tensor_copy(out=dst, in_=ps[b][:, c * 128:(c + 1) * 128]),
                                 reads=[PK(b)], writes=dkey)
                    if st == 0 and n == 0 and not NOALB:
                        P.dma(POOL, alb[:], alibi_d[:, :], reads=[], writes=["alb"])
            def C2(n):
                    t = n // 4
                    cg = st * NCH + n
                    ki = rr("kvt", 2)
                    kt = kvt[ki]
                    P.dma(SP, kt[:], kvs_d[cg, :, :], reads=[("kvs", cg)], writes=[("kvt", ki)])
                    for hf, wi in enumerate((wg0, wg1)):
                        b = tm_group(n, ring[wi], ("w", wi), 512)
                        act(gg[:, hf * 512:(hf + 1) * 512], ps[b][:], AF.Silu, [PK(b)], [("gg", hf)])
                        P.op(POOL, lambda e, hf=hf: e.tensor_tensor(out=gg[:, hf * 512:(hf + 1) * 512], in0=gg[:, hf * 512:(hf + 1) * 512],
                                                                    in1=gnb[:, hf * 512:(hf + 1) * 512], op=ALU.mult),
                             reads=[("gg", hf), "gnb"], writes=[("gg", hf)])
                    b = bank()
                    for h in range(4):
                        P.op(PE, lambda e, h=h, n=n, b=b: e.matmul(ps[b][:, h * 128:(h + 1) * 128],
                                                                  lhsT=A(SL_C + 4 + h, n * 128, (n + 1) * 128),
                                                                  rhs=A(SL_C + h, n * 128, (n + 1) * 128), start=True, stop=True),
                             reads=AK(SL_C + 4 + h, t) + AK(SL_C + h, t), writes=[PK(b)])
                    P.op(DVE, lambda e, b=b: e.tensor_tensor(out=sTb[:], in0=ps[b][:], in1=decT[:], op=ALU.mult),
                         reads=[PK(b), "decT"], writes=["sTb"])
                    bo2 = [bank(), bank()]
                    for h in range(4):
                        b = bo2[h // 2]
                        osl = slice((h % 2) * 256, (h % 2 + 1) * 256)
                        P.op(PE, lambda e, h=h, b=b, osl=osl: e.matmul(ps[b][:, osl], lhsT=sTb[:, h * 128:(h + 1) * 128],
                                                                      rhs=kt[:, 512 + h * 256: 512 + (h + 1) * 256], start=True, stop=False),
                             reads=["sTb", ("kvt", ki)], writes=[PK(b)])
                        P.op(PE, lambda e, h=h, b=b, osl=osl, n=n: e.matmul(ps[b][:, osl], lhsT=A(SL_C + h, n * 128, (n + 1) * 128),
                                                                           rhs=Sb[:, h * 256:(h + 1) * 256], start=False, stop=True),
                             reads=AK(SL_C + h, t) + ["Sb"], writes=[PK(b)])
                    for h in range(4):
                        b = bo2[h // 2]
                        osl = slice((h % 2) * 256, (h % 2 + 1) * 256)
                        act(onb[:, h * 256:(h + 1) * 256], ps[b][:, osl], AF.Copy, [PK(b)], ["onb", "bst"], accum_out=bst[:, h:h + 1])
                        act(onb[:, h * 256:(h + 1) * 256], ps[b][:, osl], AF.Square, [PK(b)], ["onb", "bst"], accum_out=bst[:, 4 + h:5 + h])
                    P.op(DVE, lambda e: e.tensor_scalar(out=mv[:, 0:4], in0=bst[:, 0:4], scalar1=1.0 / 256, scalar2=None, op0=ALU.mult),
                         reads=["bst"], writes=["mv"])
                    P.op(DVE, lambda e: e.tensor_tensor(out=bst[:, 8:12], in0=mv[:, 0:4], in1=mv[:, 0:4], op=ALU.mult),
                         reads=["mv"], writes=["bst"])
                    P.op(DVE, lambda e: e.scalar_tensor_tensor(out=mv[:, 4:8], in0=bst[:, 4:8], scalar=1.0 / 256, in1=bst[:, 8:12],
                                                               op0=ALU.mult, op1=ALU.subtract), reads=["bst"], writes=["mv"])
                    P.op(DVE, lambda e: e.tensor_tensor(out=st8[:, 56:60], in0=mv[:, 4:8], in1=epsq[:], op=ALU.add),
                         reads=["mv", "epsq"], writes=["rsC"])
                    act(st8[:, 56:60], st8[:, 56:60], AF.Ln, ["rsC"], ["rsC"])
                    act(st8[:, 56:60], st8[:, 56:60], AF.Exp, ["rsC"], ["rsC"], scale=-0.5)
                    for h in range(4):
                        b = bo2[h // 2]
                        osl = slice((h % 2) * 256, (h % 2 + 1) * 256)
                        P.op(DVE, lambda e, h=h, b=b, osl=osl: e.tensor_scalar(out=onb[:, h * 256:(h + 1) * 256], in0=ps[b][:, osl],
                                                                              scalar1=mv[:, h:h + 1], scalar2=st8[:, 56 + h:57 + h],
                                                                              op0=ALU.subtract, op1=ALU.mult),
                             reads=[PK(b), "mv", "rsC"], writes=["onb"])
                    for hf in range(2):
                        P.op(POOL, lambda e, hf=hf: e.tensor_tensor(out=yct[:, hf * 512:(hf + 1) * 512], in0=onb[:, hf * 512:(hf + 1) * 512],
                                                                    in1=gg[:, hf * 512:(hf + 1) * 512], op=ALU.mult),
                             reads=["onb", ("gg", hf)], writes=["yct"])
                    for half in range(2):
                        b = bank()
                        for cc in range(4):
                            c = half * 4 + cc
                            P.op(PE, lambda e, c=c, cc=cc, b=b: e.matmul(ps[b][:, cc * 128:(cc + 1) * 128], lhsT=yct[:, c * 128:(c + 1) * 128],
                                                                        rhs=ident_b[:], start=True, stop=True),
                                 reads=["yct", "ident_b"], writes=[PK(b)])
                        for cc in range(4):
                            c = half * 4 + cc
                            if half == 0:
                                act(A(SL_C + c, n * 128, (n + 1) * 128), ps[b][:, cc * 128:(cc + 1) * 128], AF.Copy, [PK(b)], AK(SL_C + c, t))
                            else:
                                P.op(DVE, lambda e, c=c, cc=cc, n=n, b=b: e.tensor_copy(out=A(SL_C + c, n * 128, (n + 1) * 128),
                                                                                       in_=ps[b][:, cc * 128:(cc + 1) * 128]),
                                     reads=[PK(b)], writes=AK(SL_C + c, t))
                    kv_update(kt, ("kvt", ki))
                    P.op(ACT, lambda e: e.activation(out=Sb[:], in_=S[:], func=AF.Copy), reads=["S"], writes=["Sb"])
            for n in range(NCH):
                if "A" in MIX:
                    A2(n)
                if "B" in MIX and BLV >= 2:
                    B2(n)
                if "C" in MIX:
                    C2(n)
            for kvh in range(2):
                P.op(POOL, lambda e, kvh=kvh: e.tensor_copy(out=kd(kvh, 0, 128), in_=kd(kvh, TS, TS + 128)), reads=KDK, writes=KDK)
            for m in range(8 if "M" in MIX else 0):
                wbr = wload(mw_br[l, m, :, :], 2048)
                for br, (slot0, nk, k0) in enumerate(((SL_A, 4, 0), (SL_B, 4, 4), (SL_C, 8, 8))):
                    wi = wload(mw_fm[l, 18 + br * 8 + m, :, :], 1024)
                    bg = fm_group(lambda k: ring[wi][:, k * 128:(k + 1) * 128], hT_rhs, 8, lambda k, t: [("w", wi), ("hT", t)])
                    gi = rr("gs", 2)
                    bcol = bgc[:, l * 24 + br * 8 + m: l * 24 + br * 8 + m + 1]
                    for t in range(NT):
                        act(gs[gi][:, t * 512:(t + 1) * 512], ps[bg[t]][:], AF.Sigmoid, [PK(bg[t]), "bgc"], [("gs", gi, t)],
                            bias=bcol, scale=1.0)
                    by = fm_group(lambda k: ring[wbr][:, (k0 + k) * 128:(k0 + k + 1) * 128],
                                  lambda k, t: A(slot0 + k, t * 512, (t + 1) * 512), nk,
                                  lambda k, t: [("w", wbr)] + AK(slot0 + k, t))
                    for t in range(NT):
                        if br == 0:
                            P.op(DVE, lambda e, t=t, b=by[t], gi=gi: e.tensor_tensor(out=acc[t][:], in0=ps[b][:],
                                                                                  in1=gs[gi][:, t * 512:(t + 1) * 512], op=ALU.mult),
                                 reads=[PK(by[t]), ("gs", gi, t)], writes=[("acc", t)])
                        else:
                            P.op(DVE, lambda e, t=t, b=by[t], gi=gi: e.tensor_tensor(out=tmpm[t][:], in0=ps[b][:],
                                                                                  in1=gs[gi][:, t * 512:(t + 1) * 512], op=ALU.mult),
                                 reads=[PK(by[t]), ("gs", gi, t)], writes=[("tmpm", t)])
                            if br == 1:
                                P.op(DVE, lambda e, t=t: e.tensor_tensor(out=acc[t][:], in0=acc[t][:], in1=tmpm[t][:], op=ALU.add),
                                     reads=[("acc", t), ("tmpm", t)], writes=[("acc", t)])
                            else:
                                P.op(DVE, lambda e, t=t, m=m: e.tensor_tensor(out=A(SL_M + m, t * 512, (t + 1) * 512), in0=acc[t][:],
                                                                               in1=tmpm[t][:], op=ALU.add),
                                     reads=[("acc", t), ("tmpm", t)], writes=AK(SL_M + m, t))
            if st + 1 < NST:
                norm_st(xres, "xres", st + 1, gidx)
            for m in range(8):
                wi = wload(mw_out[l, m, :, :], 1024)
                bs = fm_group(lambda k: ring[wi][:, k * 128:(k + 1) * 128],
                              lambda k, t: A(SL_M + k, t * 512, (t + 1) * 512), 8,
                              lambda k, t: [("w", wi)] + AK(SL_M + k, t))
                for t in range(NT):
                    resid(bs[t], xres, "xres", xres, "xres", st, m, t, 1.0)

    for l in range(depth):
        load_layer_consts(l)
        if l == 0:
            ffn_phase(0, xT, "xin", 0)
        else:
            ffn_phase(2 * l, xres, "xres", l * 3)
        if "pass1" in STAGES:
            pass1(l)
        if "mixer" in STAGES:
            mixer_phase(l)
        if "ffn2" in STAGES:
            ffn_phase(2 * l + 1, xres, "xres", l * 3 + 2)
    if do_final:
        for st in range(NST):
            for t in range(NT):
                norm_sub(xres, XK("xres", st), st, t, 3 * L, final_dst=outT)
    else:
        for st in range(NST):
            for t in range(NT):
                for k in range(8):
                    tok0 = st * TS + t * 512
                    i = rr("xr", 2)
                    P.dma(SP, xr[i][:], xres[k, :, tok0:tok0 + 512], reads=[("xres", st, k, t)], writes=[("xr", i)])
                    fin.append(P.dma(SP, outT[k, :, tok0:tok0 + 512], xr[i][:], reads=[("xr", i)], writes=[("out", st, k, t)]))
    P.wait_all(SP, fin)
    P.emit()
    return nc


def prep_inputs(inp, cst):
    f = lambda a: np.ascontiguousarray(np.asarray(a, dtype=np.float32))
    sh = {}
    g = np.zeros((128, (3 * L + 1) * 8), np.float32)
    for l in range(L):
        for wi, nm in enumerate(("ffn1_norm", "mix_norm", "ffn2_norm")):
            g[:, (l * 3 + wi) * 8:(l * 3 + wi + 1) * 8] = f(inp[nm])[l].reshape(8, 128).T
    g[:, 3 * L * 8:] = f(inp["final_norm"]).reshape(8, 128).T
    sh["gains"] = g
    win = np.empty((2 * L, NJ, 128, 2048), np.float32)
    wout = np.empty((2 * L, 8, 128, FF), np.float32)
    for l in range(L):
        for wi, (a, b) in enumerate((("ffn1_w_in", "ffn1_w_out"), ("ffn2_w_in", "ffn2_w_out"))):
            Wi = f(inp[a])[l]
            cols = [np.concatenate([np.arange(j * 128, (j + 1) * 128), np.arange(FF + j * 128, FF + (j + 1) * 128)]) for j in range(NJ)]
            win[l * 2 + wi] = _chunks_lhs(Wi, cols)
            Wo = f(inp[b])[l]
            wout[l * 2 + wi] = _chunks_lhs(Wo, [np.arange(m * 128, (m + 1) * 128) for m in range(8)])
    sh["ffn_win"], sh["ffn_wout"] = win, wout
    fmc = _fm_cols()
    sh["mw_fm"] = np.stack([_chunks_lhs(f(inp["w_in"])[l], fmc) for l in range(L)])
    tmc = [np.arange(O_AV, O_AV + 512), np.arange(O_CG, O_CG + 512), np.arange(O_CG + 512, O_CG + 1024),
           np.arange(O_CK, O_CK + 512), np.arange(O_CV, O_CV + 512), np.arange(O_CV + 512, O_CV + 1024)]
    sh["mw_tm"] = np.stack([_chunks_lhs(f(inp["w_in"])[l], tmc) for l in range(L)])
    sh["mw_attv"] = np.stack([_chunks_lhs(f(inp["w_in"])[l], [np.arange(O_BV, O_BV + 128)])[0] for l in range(L)])
    mcols = [np.arange(m * 128, (m + 1) * 128) for m in range(8)]
    sh["mw_br"] = np.stack([_chunks_lhs(np.concatenate([f(inp["w_branch_a"])[l], f(inp["w_branch_b"])[l], f(inp["w_branch_c"])[l]], 0), mcols)
                            for l in range(L)])
    sh["mw_out"] = np.stack([_chunks_lhs(f(inp["w_out"])[l], mcols) for l in range(L)])
    bg = np.zeros((128, L * 24), np.float32)
    for l in range(L):
        bg[:, l * 24:(l + 1) * 24] = f(inp["b_gate"])[l].reshape(24, 128).T
    sh["bgate"] = bg
    sh["vng"] = np.ascontiguousarray(np.broadcast_to(f(inp["gmlp_v_norm"])[:, None, :], (L, 128, 512)))
    sh["bsb"] = np.ascontiguousarray(np.broadcast_to(f(inp["gmlp_b_s"]).reshape(L, 1, 512), (L, 128, 512)))
    sh["wsT"] = np.ascontiguousarray(f(inp["gmlp_w_s"]).transpose(0, 3, 1, 2).reshape(L, 128, 512))
    sh["sinkb"] = np.ascontiguousarray(np.broadcast_to(f(inp["attn_sinks"])[:, None, :], (L, 128, 8)))
    sh["gnb"] = np.ascontiguousarray(np.broadcast_to(f(inp["ret_gn"])[:, None, :], (L, 128, 1024)))
    for k in ("cmask", "ident", "alibi", "decT", "epsq", "kw"):
        sh[k] = cst[k]
    return sh


_CACHE = {}


def run(inp, depth=L, do_final=True, ncores=NCORE):
    cst = _consts()
    key = (depth, do_final)
    if key not in _CACHE:
        _CACHE[key] = build(depth, do_final, cst)
    nc = _CACHE[key]
    sh = prep_inputs(inp, cst)
    x = np.asarray(inp["x"], dtype=np.float32)
    in_maps = []
    for c in range(NCORE):
        b, j = c // 4, c % 4
        m = dict(sh)
        m["xT"] = np.ascontiguousarray(x[b, j * TOK:(j + 1) * TOK, :].T).reshape(8, 128, TOK)
        m.update(_core_consts(c, cst))
        in_maps.append(m)
    res = run_bass_kernel_spmd(nc, in_maps[:ncores], core_ids=list(range(ncores)))
    out = np.zeros((2, 4 * TOK, D), np.float32)
    for c in range(ncores):
        b, j = c // 4, c % 4
        out[b, j * TOK:(j + 1) * TOK, :] = res.results[c]["outT"].reshape(D, TOK).T
    return out


def kernel(**inputs):
    return run(inputs)
```
